# Optimizing a Trainium2 kernel written in Bass

```python
import math
import jax
import jax.numpy as jnp
from jax import lax
import numpy as np

D_MODEL = 1024
BATCH = 2
SEQ = 8192
DEPTH = 4

GRID_W = 64
CTX_LEN = 256
GROUP_W = 256
D_MIX = 4 * GROUP_W
D_FF = 4 * D_MODEL
N_MOD = 6
Q_BLOCK = 128
ROPE_THETA = 10000.0
NORM_EPS = 1e-6
CONV_W = 4

SSD_HEADS = 4
SSD_HEAD_DIM = GROUP_W // SSD_HEADS
SSD_GROUPS = 2
SSD_STATE = 128
SSD_CHUNK = 128
SSD_XBC = GROUP_W + 2 * SSD_GROUPS * SSD_STATE
SSD_COLS = GROUP_W + SSD_XBC + 2 * SSD_HEADS

GQA_HEADS = 4
GQA_KV_HEADS = 2
GQA_HEAD_DIM = GROUP_W // GQA_HEADS
GQA_COLS = GROUP_W + 2 * GQA_KV_HEADS * GQA_HEAD_DIM

LRU_WIDTH = GROUP_W
LRU_BLOCKS = 4
LRU_BLOCK_W = LRU_WIDTH // LRU_BLOCKS
LRU_C = 8.0
LRU_COLS = 2 * LRU_WIDTH

DIFF_HEADS = 4
DIFF_V_DIM = GROUP_W // DIFF_HEADS
DIFF_QK_DIM = DIFF_V_DIM // 2
DIFF_COLS = 3 * GROUP_W

IN_COLS = SSD_COLS + GQA_COLS + LRU_COLS + DIFF_COLS
IN_SPLITS = (SSD_COLS, SSD_COLS + GQA_COLS, SSD_COLS + GQA_COLS + LRU_COLS)

kernel_name = 'hybrid_parallel_heads_diffusion_trunk'


def rms_norm(x, g):
    xf = x.astype(jnp.float32)
    y = xf * lax.rsqrt(jnp.mean(xf * xf, axis=-1, keepdims=True) + NORM_EPS)
    return (y * g.astype(jnp.float32)).astype(x.dtype)


def modulate(x, shift, scale):
    return x * (1 + scale) + shift


def dwconv(x, w, b):
    k = w.shape[0]
    pad_l = k // 2
    y = lax.conv_general_dilated(x, w[:, None, :], window_strides=(1,), padding=[(pad_l, k - 1 - pad_l)],
                                 dimension_numbers=('NWC', 'WIO', 'NWC'), feature_group_count=x.shape[-1])
    return y + b


def _rotate(x, pos):
    half = x.shape[-1] // 2
    freqs = ROPE_THETA ** (-jnp.arange(half, dtype=jnp.float32) / half)
    ang = pos.astype(jnp.float32)[:, None] * freqs[None, :]
    cos = jnp.cos(ang)[None, :, None, :]
    sin = jnp.sin(ang)[None, :, None, :]
    x1, x2 = x[..., :half], x[..., half:]
    return jnp.concatenate([x1 * cos - x2 * sin, x1 * sin + x2 * cos], axis=-1)


def axial_rope(x, row, col):
    m = x.shape[-1] // 2
    xf = x.astype(jnp.float32)
    return jnp.concatenate([_rotate(xf[..., :m], row), _rotate(xf[..., m:], col)], axis=-1).astype(x.dtype)


def block_attention(q, k, v):
    b, lq, hq, d = q.shape
    hkv, dv = k.shape[2], v.shape[-1]
    rep = hq // hkv
    nb = lq // Q_BLOCK
    qb = jnp.moveaxis(q.reshape(b, nb, Q_BLOCK, hkv, rep, d), 1, 0)
    scale = d ** -0.5

    def one_block(qblk):
        s = jnp.einsum('bqgrd,bkgd->bgrqk', qblk, k, preferred_element_type=jnp.float32) * scale
        pr = jax.nn.softmax(s, axis=-1).astype(v.dtype)
        return jnp.einsum('bgrqk,bkgv->bqgrv', pr, v)

    o = lax.map(one_block, qb)
    return jnp.moveaxis(o, 0, 1).reshape(b, lq, hq, dv)


def ssd_chunked_scan(x, dt, a, bm, cm, h0):
    b, L, H, P = x.shape
    nc = L // SSD_CHUNK

    def chunk(t):
        return t.reshape((b, nc, SSD_CHUNK) + t.shape[2:])

    x, dt, bm, cm = chunk(x), chunk(dt), chunk(bm), chunk(cm)
    a_cum = jnp.cumsum(dt * a, axis=2)
    xdt = x * dt[..., None]
    seg = a_cum[:, :, :, None, :] - a_cum[:, :, None, :, :]
    lower = jnp.tril(jnp.ones((SSD_CHUNK, SSD_CHUNK), dtype=bool))[None, None, :, :, None]
    decay = jnp.exp(jnp.where(lower, seg, -jnp.inf))
    scores = jnp.einsum('bclhn,bcshn->bclsh', cm, bm) * decay
    y_diag = jnp.einsum('bclsh,bcshp->bclhp', scores, xdt)
    to_end = jnp.exp(a_cum[:, :, -1:, :] - a_cum)
    states = jnp.einsum('bclhn,bclhp->bchpn', bm * to_end[..., None], xdt)
    chunk_decay = jnp.exp(a_cum[:, :, -1, :])

    def step(h, inp):
        s, dcy = inp
        return h * dcy[:, :, None, None] + s, h

    h_last, h_in = lax.scan(step, h0, (jnp.moveaxis(states, 1, 0), jnp.moveaxis(chunk_decay, 1, 0)))
    h_in = jnp.moveaxis(h_in, 0, 1)
    y_off = jnp.einsum('bclhn,bchpn->bclhp', cm * jnp.exp(a_cum)[..., None], h_in)
    return (y_diag + y_off).reshape(b, L, H, P), h_last


def ssd_features(proj, conv_w, conv_b):
    b, L, _ = proj.shape
    z, xbc, dt = jnp.split(proj, [GROUP_W, GROUP_W + SSD_XBC], axis=-1)
    xbc = jax.nn.silu(dwconv(xbc, conv_w, conv_b)).astype(jnp.float32)
    xs, bm, cm = jnp.split(xbc, [GROUP_W, GROUP_W + SSD_GROUPS * SSD_STATE], axis=-1)
    rep = SSD_HEADS // SSD_GROUPS
    xs = xs.reshape(b, L, SSD_HEADS, SSD_HEAD_DIM)
    bm = jnp.repeat(bm.reshape(b, L, SSD_GROUPS, SSD_STATE), rep, axis=2)
    cm = jnp.repeat(cm.reshape(b, L, SSD_GROUPS, SSD_STATE), rep, axis=2)
    dt = dt.astype(jnp.float32).reshape(b, L, 2, SSD_HEADS)
    return z, xs, bm, cm, dt


def ssd_direction(xs, bm, cm, dt_raw, a_log, dt_bias, d_skip, h0, reverse):
    dt = jax.nn.softplus(dt_raw + dt_bias.astype(jnp.float32))
    a = -jnp.exp(a_log.astype(jnp.float32))
    if reverse:
        y, h = ssd_chunked_scan(jnp.flip(xs, 1), jnp.flip(dt, 1), a, jnp.flip(bm, 1), jnp.flip(cm, 1), h0)
        y = jnp.flip(y, 1)
    else:
        y, h = ssd_chunked_scan(xs, dt, a, bm, cm, h0)
    return y + d_skip.astype(jnp.float32)[:, None] * xs, h


def ssd_bidir(feats, a_log, dt_bias, d_skip, norm_g, h0_fwd, h0_bwd):
    z, xs, bm, cm, dt = feats
    b, L = xs.shape[:2]
    y_f, h_f = ssd_direction(xs, bm, cm, dt[:, :, 0], a_log[0], dt_bias[0], d_skip[0], h0_fwd, False)
    y_b, h_b = ssd_direction(xs, bm, cm, dt[:, :, 1], a_log[1], dt_bias[1], d_skip[1], h0_bwd, True)
    y = (y_f + y_b).reshape(b, L, GROUP_W) * jax.nn.silu(z.astype(jnp.float32))
    y = rms_norm(y.reshape(b, L, SSD_GROUPS, GROUP_W // SSD_GROUPS),
                 norm_g.reshape(SSD_GROUPS, GROUP_W // SSD_GROUPS)).reshape(b, L, GROUP_W)
    return y.astype(z.dtype), h_f, h_b


def lru_features(proj, conv_w, conv_b):
    gate, xr = jnp.split(proj, 2, axis=-1)
    return jax.nn.gelu(gate, approximate=True), dwconv(xr, conv_w, conv_b).astype(jnp.float32)


def block_diag(x, w, bias):
    b, L, _ = x.shape
    y = jnp.einsum('blnj,njk->blnk', x.reshape(b, L, LRU_BLOCKS, LRU_BLOCK_W), w.astype(jnp.float32))
    return y.reshape(b, L, LRU_WIDTH) + bias.astype(jnp.float32)


def linear_scan(a, u, h0):
    u = u.at[:, 0].add(a[:, 0] * h0)

    def combine(left, right):
        return left[0] * right[0], right[0] * left[1] + right[1]

    _, h = lax.associative_scan(combine, (a, u), axis=1)
    return h


def rglru_direction(xr, w_r, b_r, w_i, b_i, lam, h0, reverse):
    if reverse:
        xr = jnp.flip(xr, 1)
    r = jax.nn.sigmoid(block_diag(xr, w_r, b_r))
    i = jax.nn.sigmoid(block_diag(xr, w_i, b_i))
    log_a = -LRU_C * r * jax.nn.softplus(-lam.astype(jnp.float32))
    a = jnp.exp(log_a)
    u = jnp.sqrt(-jnp.expm1(2.0 * log_a)) * (i * xr)
    h = linear_scan(a, u, h0)
    h_last = h[:, -1]
    if reverse:
        h = jnp.flip(h, 1)
    return h, h_last


def gqa_qkv(proj, q_g, k_g):
    b, L, _ = proj.shape
    q, k, v = jnp.split(proj, [GROUP_W, GROUP_W + GQA_KV_HEADS * GQA_HEAD_DIM], axis=-1)
    q = rms_norm(q.reshape(b, L, GQA_HEADS, GQA_HEAD_DIM), q_g)
    k = rms_norm(k.reshape(b, L, GQA_KV_HEADS, GQA_HEAD_DIM), k_g)
    return q, k, v.reshape(b, L, GQA_KV_HEADS, GQA_HEAD_DIM)


def diff_qkv(proj, q_g, k_g):
    b, L, _ = proj.shape
    q, k, v = jnp.split(proj, 3, axis=-1)
    q = rms_norm(q.reshape(b, L, 2 * DIFF_HEADS, DIFF_QK_DIM), q_g)
    k = rms_norm(k.reshape(b, L, 2 * DIFF_HEADS, DIFF_QK_DIM), k_g)
    return q, k, v.reshape(b, L, DIFF_HEADS, DIFF_V_DIM)


def diff_attention(q, k, v, lam, lam_init, subln_g):
    o1 = block_attention(q[:, :, 0::2], k[:, :, 0::2], v)
    o2 = block_attention(q[:, :, 1::2], k[:, :, 1::2], v)
    o = (o1.astype(jnp.float32) - lam * o2.astype(jnp.float32))
    o = rms_norm(o, subln_g) * (1.0 - lam_init)
    b, L = o.shape[:2]
    return o.reshape(b, L, GROUP_W).astype(v.dtype)


def sq_relu_mlp(u, w1, w2):
    return jnp.square(jax.nn.relu(u @ w1)) @ w2


def trunk_layer(h_lat, h_ctx, c_act, cctx_act, p, row, col, lam_init, update_ctx):
    bsz, seq_len, _ = h_lat.shape
    dtype = h_lat.dtype
    mod_l = jnp.split((c_act @ p['w_mod'] + p['b_mod'])[:, None, :], N_MOD, axis=-1)
    mod_c = jnp.split(cctx_act @ p['w_mod'] + p['b_mod'], N_MOD, axis=-1)

    u_l = modulate(rms_norm(h_lat, p['norm1_g']), mod_l[0], mod_l[1])
    u_c = modulate(rms_norm(h_ctx, p['norm1_g']), mod_c[0], mod_c[1])
    pa_l, pb_l, pc_l, pd_l = jnp.split(u_l @ p['w_in'], IN_SPLITS, axis=-1)
    pa_c, pb_c, pc_c, pd_c = jnp.split(u_c @ p['w_in'], IN_SPLITS, axis=-1)

    feats_c = ssd_features(pa_c, p['ssd_conv_w'], p['ssd_conv_b'])
    feats_l = ssd_features(pa_l, p['ssd_conv_w'], p['ssd_conv_b'])
    h0 = jnp.zeros((bsz, SSD_HEADS, SSD_HEAD_DIM, SSD_STATE), jnp.float32)
    ya_c, hs_f, hs_b = ssd_bidir(feats_c, p['ssd_a_log'], p['ssd_dt_bias'], p['ssd_d'], p['ssd_norm_g'], h0, h0)
    ya_l, _, _ = ssd_bidir(feats_l, p['ssd_a_log'], p['ssd_dt_bias'], p['ssd_d'], p['ssd_norm_g'], hs_f, hs_b)

    qb_c, kb_c, vb_c = gqa_qkv(pb_c, p['gqa_q_norm_g'], p['gqa_k_norm_g'])
    qb_l, kb_l, vb_l = gqa_qkv(pb_l, p['gqa_q_norm_g'], p['gqa_k_norm_g'])
    qb_l, kb_l = axial_rope(qb_l, row, col), axial_rope(kb_l, row, col)
    yb_l = block_attention(qb_l, jnp.concatenate([kb_c, kb_l], axis=1),
                           jnp.concatenate([vb_c, vb_l], axis=1)).reshape(bsz, seq_len, GROUP_W)

    gc_c, xc_c = lru_features(pc_c, p['lru_conv_w'], p['lru_conv_b'])
    gc_l, xc_l = lru_features(pc_l, p['lru_conv_w'], p['lru_conv_b'])
    hz = jnp.zeros((bsz, LRU_WIDTH), jnp.float32)
    rc_f, hl_f = rglru_direction(xc_c, p['lru_w_r'][0], p['lru_b_r'][0], p['lru_w_i'][0], p['lru_b_i'][0], p['lru_lambda'][0], hz, False)
    rc_b, hl_b = rglru_direction(xc_c, p['lru_w_r'][1], p['lru_b_r'][1], p['lru_w_i'][1], p['lru_b_i'][1], p['lru_lambda'][1], hz, True)
    rl_f, _ = rglru_direction(xc_l, p['lru_w_r'][0], p['lru_b_r'][0], p['lru_w_i'][0], p['lru_b_i'][0], p['lru_lambda'][0], hl_f, False)
    rl_b, _ = rglru_direction(xc_l, p['lru_w_r'][1], p['lru_b_r'][1], p['lru_w_i'][1], p['lru_b_i'][1], p['lru_lambda'][1], hl_b, True)
    yc_l = (gc_l * (rl_f + rl_b)).astype(dtype)

    lam = (jnp.exp(jnp.sum(p['diff_lambda_q1'] * p['diff_lambda_k1']).astype(jnp.float32))
           - jnp.exp(jnp.sum(p['diff_lambda_q2'] * p['diff_lambda_k2']).astype(jnp.float32)) + lam_init)
    qd_c, kd_c, vd_c = diff_qkv(pd_c, p['diff_q_norm_g'], p['diff_k_norm_g'])
    qd_l, kd_l, vd_l = diff_qkv(pd_l, p['diff_q_norm_g'], p['diff_k_norm_g'])
    qd_l, kd_l = axial_rope(qd_l, row, col), axial_rope(kd_l, row, col)
    yd_l = diff_attention(qd_l, jnp.concatenate([kd_c, kd_l], axis=1), jnp.concatenate([vd_c, vd_l], axis=1),
                          lam, lam_init, p['diff_subln_g'])

    mix_l = jnp.concatenate([ya_l, yb_l, yc_l, yd_l], axis=-1) @ p['w_out']
    h_lat = h_lat + mod_l[2] * mix_l
    v_l = modulate(rms_norm(h_lat, p['norm2_g']), mod_l[3], mod_l[4])
    h_lat = h_lat + mod_l[5] * sq_relu_mlp(v_l, p['w_mlp1'], p['w_mlp2'])

    if update_ctx:
        yb_c = block_attention(qb_c, kb_c, vb_c).reshape(bsz, CTX_LEN, GROUP_W)
        yc_c = (gc_c * (rc_f + rc_b)).astype(dtype)
        yd_c = diff_attention(qd_c, kd_c, vd_c, lam, lam_init, p['diff_subln_g'])
        mix_c = jnp.concatenate([ya_c, yb_c, yc_c, yd_c], axis=-1) @ p['w_out']
        h_ctx = h_ctx + mod_c[2] * mix_c
        v_c = modulate(rms_norm(h_ctx, p['norm2_g']), mod_c[3], mod_c[4])
        h_ctx = h_ctx + mod_c[5] * sq_relu_mlp(v_c, p['w_mlp1'], p['w_mlp2'])
    return h_lat, h_ctx


def setup_inputs(seed: int = 0) -> dict:
    key = jax.random.key(seed)
    keys = iter(jax.random.split(key, 48))

    def normal(shape, scale):
        return scale * jax.random.normal(next(keys), shape, jnp.float32)

    def gain(shape):
        return 1.0 + normal(shape, 0.05)

    def uniform(shape, lo, hi):
        return jax.random.uniform(next(keys), shape, jnp.float32, lo, hi)

    L = DEPTH
    x = normal((BATCH, SEQ, D_MODEL), 1.0)
    c = normal((BATCH, D_MODEL), 1.0)
    ctx = normal((BATCH, CTX_LEN, D_MODEL), 1.0)
    c_ctx = normal((D_MODEL,), 1.0)
    w_mod = normal((L, D_MODEL, N_MOD * D_MODEL), 0.5 * D_MODEL ** -0.5)
    b_mod = normal((L, N_MOD * D_MODEL), 0.02)
    norm1_g = gain((L, D_MODEL))
    w_in = normal((L, D_MODEL, IN_COLS), D_MODEL ** -0.5)
    ssd_conv_w = normal((L, CONV_W, SSD_XBC), CONV_W ** -0.5)
    ssd_conv_b = normal((L, SSD_XBC), 0.02)
    ssd_a_log = jnp.log(uniform((L, 2, SSD_HEADS), 1.0, 16.0))
    dt0 = jnp.exp(uniform((L, 2, SSD_HEADS), math.log(1e-3), math.log(1e-1)))
    ssd_dt_bias = dt0 + jnp.log(-jnp.expm1(-dt0))
    ssd_d = gain((L, 2, SSD_HEADS))
    ssd_norm_g = gain((L, GROUP_W))
    gqa_q_norm_g = gain((L, GQA_HEAD_DIM))
    gqa_k_norm_g = gain((L, GQA_HEAD_DIM))
    lru_conv_w = normal((L, CONV_W, LRU_WIDTH), CONV_W ** -0.5)
    lru_conv_b = normal((L, LRU_WIDTH), 0.02)
    lru_w_r = normal((L, 2, LRU_BLOCKS, LRU_BLOCK_W, LRU_BLOCK_W), LRU_BLOCK_W ** -0.5)
    lru_b_r = normal((L, 2, LRU_WIDTH), 0.02)
    lru_w_i = normal((L, 2, LRU_BLOCKS, LRU_BLOCK_W, LRU_BLOCK_W), LRU_BLOCK_W ** -0.5)
    lru_b_i = normal((L, 2, LRU_WIDTH), 0.02)
    a_pow = uniform((L, 2, LRU_WIDTH), 0.9, 0.999)
    a_base = a_pow ** (1.0 / LRU_C)
    lru_lambda = jnp.log(a_base) - jnp.log1p(-a_base)
    diff_q_norm_g = gain((L, DIFF_QK_DIM))
    diff_k_norm_g = gain((L, DIFF_QK_DIM))
    diff_lambda_q1 = normal((L, DIFF_QK_DIM), 0.1)
    diff_lambda_k1 = normal((L, DIFF_QK_DIM), 0.1)
    diff_lambda_q2 = normal((L, DIFF_QK_DIM), 0.1)
    diff_lambda_k2 = normal((L, DIFF_QK_DIM), 0.1)
    diff_subln_g = gain((L, DIFF_V_DIM))
    w_out = normal((L, D_MIX, D_MODEL), D_MIX ** -0.5)
    norm2_g = gain((L, D_MODEL))
    w_mlp1 = normal((L, D_MODEL, D_FF), D_MODEL ** -0.5)
    w_mlp2 = normal((L, D_FF, D_MODEL), D_FF ** -0.5)
    return {'x': x, 'c': c, 'ctx': ctx, 'c_ctx': c_ctx, 'w_mod': w_mod, 'b_mod': b_mod, 'norm1_g': norm1_g,
            'w_in': w_in, 'ssd_conv_w': ssd_conv_w, 'ssd_conv_b': ssd_conv_b, 'ssd_a_log': ssd_a_log,
            'ssd_dt_bias': ssd_dt_bias, 'ssd_d': ssd_d, 'ssd_norm_g': ssd_norm_g, 'gqa_q_norm_g': gqa_q_norm_g,
            'gqa_k_norm_g': gqa_k_norm_g, 'lru_conv_w': lru_conv_w, 'lru_conv_b': lru_conv_b, 'lru_w_r': lru_w_r,
            'lru_b_r': lru_b_r, 'lru_w_i': lru_w_i, 'lru_b_i': lru_b_i, 'lru_lambda': lru_lambda,
            'diff_q_norm_g': diff_q_norm_g, 'diff_k_norm_g': diff_k_norm_g, 'diff_lambda_q1': diff_lambda_q1,
            'diff_lambda_k1': diff_lambda_k1, 'diff_lambda_q2': diff_lambda_q2, 'diff_lambda_k2': diff_lambda_k2,
            'diff_subln_g': diff_subln_g, 'w_out': w_out, 'norm2_g': norm2_g, 'w_mlp1': w_mlp1, 'w_mlp2': w_mlp2}


def reference(x, c, ctx, c_ctx, w_mod, b_mod, norm1_g, w_in, ssd_conv_w, ssd_conv_b, ssd_a_log, ssd_dt_bias,
              ssd_d, ssd_norm_g, gqa_q_norm_g, gqa_k_norm_g, lru_conv_w, lru_conv_b, lru_w_r, lru_b_r, lru_w_i,
              lru_b_i, lru_lambda, diff_q_norm_g, diff_k_norm_g, diff_lambda_q1, diff_lambda_k1, diff_lambda_q2,
              diff_lambda_k2, diff_subln_g, w_out, norm2_g, w_mlp1, w_mlp2):
    n_tok = x.shape[1]
    ROWS = n_tok // GRID_W
    row = jnp.repeat(jnp.arange(ROWS, dtype=jnp.int32), GRID_W)
    col = jnp.tile(jnp.arange(GRID_W, dtype=jnp.int32), ROWS)
    c_act = jax.nn.silu(c)
    cctx_act = jax.nn.silu(c_ctx)
    h_lat, h_ctx = x, ctx
    for l in range(DEPTH):
        p = {'w_mod': w_mod[l], 'b_mod': b_mod[l], 'norm1_g': norm1_g[l], 'w_in': w_in[l],
             'ssd_conv_w': ssd_conv_w[l], 'ssd_conv_b': ssd_conv_b[l], 'ssd_a_log': ssd_a_log[l],
             'ssd_dt_bias': ssd_dt_bias[l], 'ssd_d': ssd_d[l], 'ssd_norm_g': ssd_norm_g[l],
             'gqa_q_norm_g': gqa_q_norm_g[l], 'gqa_k_norm_g': gqa_k_norm_g[l],
             'lru_conv_w': lru_conv_w[l], 'lru_conv_b': lru_conv_b[l], 'lru_w_r': lru_w_r[l], 'lru_b_r': lru_b_r[l],
             'lru_w_i': lru_w_i[l], 'lru_b_i': lru_b_i[l], 'lru_lambda': lru_lambda[l],
             'diff_q_norm_g': diff_q_norm_g[l], 'diff_k_norm_g': diff_k_norm_g[l],
             'diff_lambda_q1': diff_lambda_q1[l], 'diff_lambda_k1': diff_lambda_k1[l],
             'diff_lambda_q2': diff_lambda_q2[l], 'diff_lambda_k2': diff_lambda_k2[l],
             'diff_subln_g': diff_subln_g[l], 'w_out': w_out[l], 'norm2_g': norm2_g[l],
             'w_mlp1': w_mlp1[l], 'w_mlp2': w_mlp2[l]}
        lam_init = 0.8 - 0.6 * math.exp(-0.3 * l)
        h_lat, h_ctx = trunk_layer(h_lat, h_ctx, c_act, cctx_act, p, row, col, lam_init, l < DEPTH - 1)
    return h_lat
```

```python
import math
from contextlib import ExitStack
import numpy as np
import ml_dtypes
import concourse.bass as bass
import concourse.mybir as mybir
from concourse.bass_utils import run_bass_kernel_spmd

F32 = mybir.dt.float32
BF16 = mybir.dt.bfloat16
U8 = mybir.dt.uint8
AF = mybir.ActivationFunctionType
ALU = mybir.AluOpType
AX = mybir.AxisListType
NPBF = ml_dtypes.bfloat16

D = 1024
KC = 8
CTX = 256
NCT = 2
DEPTH = 4
GRID_W = 64
EPS = 1e-6
ENG = ("pe", "act", "dve", "pool", "sp")
NDSEM = 24
NHW = 16


class Op:
    __slots__ = ("eng", "fn", "deps", "sig", "val", "isdma", "dsem", "dval", "ispe")


class Prog:
    def __init__(self, nc):
        self.nc = nc
        self.ops = {e: [] for e in ENG}
        self.lastw = {}
        self.readers = {}
        self.dma_rr = 0
        self.sw_rr = 0
        self.dsem_last = [None] * NDSEM
        self.last_op = {e: None for e in ENG}
        self.all_dma = []

    def add(self, eng, fn, r=(), w=(), isdma=False):
        op = Op()
        op.eng = eng; op.fn = fn; op.isdma = isdma; op.sig = False; op.val = 0
        op.ispe = (eng == "pe" and not isdma)
        deps = []
        for k in r:
            x = self.lastw.get(k)
            if x is not None:
                deps.append(x)
        for k in w:
            x = self.lastw.get(k)
            if x is not None:
                deps.append(x)
            deps.extend(self.readers.get(k, ()))
        if isdma:
            if eng == "pool":
                i = NHW + self.sw_rr
                self.sw_rr = (self.sw_rr + 1) % (NDSEM - NHW)
            else:
                i = self.dma_rr
                self.dma_rr = (i + 1) % NHW
            prev = self.dsem_last[i]
            if prev is not None:
                deps.append(prev)
            op.dsem = i
            op.dval = (prev.dval if prev is not None else 0) + 16
            self.dsem_last[i] = op
            self.all_dma.append(op)
        seen = set(); dd = []
        for d in deps:
            if id(d) not in seen and d is not op:
                seen.add(id(d)); dd.append(d)
        op.deps = dd
        for k in w:
            self.lastw[k] = op
            self.readers[k] = []
        for k in r:
            if k not in w:
                self.readers.setdefault(k, []).append(op)
        self.ops[eng].append(op)
        self.last_op[eng] = op
        return op

    def barrier(self):
        lasts = [self.last_op[e] for e in ENG if self.last_op[e] is not None]
        dm = [d for d in self.dsem_last if d is not None]
        for e in ENG:
            op = self.add(e, None)
            op.deps = [d for d in lasts + dm if d is not op]
        self.lastw = {}
        self.readers = {}

    def finish(self, out_keys):
        op = self.add("sp", None, r=list(out_keys))
        return op

    def emit(self, block, csem, dsem):
        for e in ENG:
            for op in self.ops[e]:
                for d in op.deps:
                    if d.isdma:
                        continue
                    if d.ispe and op.ispe:
                        continue
                    d.sig = True
        for e in ENG:
            c = 0
            for op in self.ops[e]:
                if (not op.isdma) and op.sig:
                    if op.fn is None:
                        op.sig = False
                        continue
                    c += 1
                    op.val = c
        nc = self.nc
        engobj = {"pe": "tensor", "act": "scalar", "dve": "vector", "pool": "gpsimd", "sp": "sync"}

        def run(e):
            def body(eng):
                waited = {}
                for op in self.ops[e]:
                    need = {}
                    for d in op.deps:
                        if d.isdma:
                            key = ("d", d.dsem); v = d.dval
                        else:
                            if d.ispe and op.ispe:
                                continue
                            if d.fn is None:
                                continue
                            key = ("c", d.eng); v = d.val
                        if need.get(key, 0) < v:
                            need[key] = v
                    for key, v in need.items():
                        if waited.get(key, 0) < v:
                            sem = dsem[key[1]] if key[0] == "d" else csem[key[1]]
                            eng.wait_ge(sem, v)
                            waited[key] = v
                    if op.fn is None:
                        continue
                    ins = op.fn(eng)
                    if op.isdma:
                        ins.then_inc(dsem[op.dsem], 16)
                    elif op.sig:
                        ins.then_inc(csem[e], 1)
            return body

        block.tensor(run("pe"))
        block.scalar(run("act"))
        block.vector(run("dve"))
        block.gpsimd(run("pool"))
        block.sync(run("sp"))


class Arena:
    def __init__(self, ap, nbytes):
        self.ap = ap; self.n = nbytes; self.off = 0; self.peak = 0

    def alloc(self, shape, dtype, parts=128):
        es = 2 if dtype == BF16 else 4
        free = 1
        for s in shape[1:]:
            free *= s
        nb = (free * es + 63) // 64 * 64
        assert self.off + nb <= self.n, f"arena overflow {self.off}+{nb}>{self.n}"
        v = self.ap[0:shape[0], self.off:self.off + free * es].bitcast(dtype)
        self.off += nb
        self.peak = max(self.peak, self.off)
        if len(shape) == 3:
            v = v.rearrange("p (a b) -> p a b", a=shape[1])
        elif len(shape) == 4:
            v = v.rearrange("p (a b c) -> p a b c", a=shape[1], b=shape[2])
        return v

    def mark(self):
        return self.off

    def release(self, m):
        self.off = m


def blocks_of(n, bw=512):
    return [(o, min(bw, n - o)) for o in range(0, n, bw)]


def build_program(T, phase, dbg=None):
    NT = T // 128
    NTT = NCT + NT
    TT = CTX + T
    LO = CTX + 2
    UW = CTX + T + 4
    LK = CTX + 4 * T
    NK = LK // 128
    QB = min(512, T)
    nc = bass.Bass("TRN2", target_bir_lowering=False)
    dins = {}
    douts = {}

    def din(name, shape, dt=F32):
        dins[name] = nc.dram_tensor(name, list(shape), dt, kind="ExternalInput").ap()
        return dins[name]

    def dout(name, shape, dt=F32):
        douts[name] = nc.dram_tensor(name, list(shape), dt, kind="ExternalOutput").ap()
        return douts[name]

    hT_d = din("hT", [KC, 128, TT]); halo_d = din("halo", [KC, 128, 4]); hmask_d = din("hmask", [128, 4])
    cT_d = din("cT", [128, KC, 2]); wmod_d = din("w_mod", [D, 6 * D]); bmod_d = din("b_modT", [128, 48])
    g1_d = din("g1", [128, KC]); g2_d = din("g2", [128, KC]); win_d = din("w_in", [D, 2824])
    cws_d = din("convw_s", [128, 6, 4]); cbs_d = din("convb_s", [128, 6])
    cwl_d = din("convw_l", [128, 2, 4]); cbl_d = din("convb_l", [128, 2])
    alog_d = din("alog", [8, 1]); dtb_d = din("dtb", [8, 1])
    ssdd_d = din("ssd_d_bc", [128, 2, 256]); ssdng_d = din("ssd_ng_bc", [128, 256])
    gqkg_d = din("gqk_g", [128, 384]); dqkg_d = din("dqk_g", [128, 512])
    lamv_d = din("lamv", [128, 4, 32]); lami_d = din("laminit", [128, 1]); subln_d = din("subln_bc", [128, 64])
    lruw_d = din("lru_w", [128, 8, 128]); lrub_d = din("lru_b", [128, 8]); lrul_d = din("lru_lam", [128, 4])
    wout_d = din("w_out", [D, D]); w1_d = din("w_mlp1", [D, 4 * D]); w2_d = din("w_mlp2", [4 * D, D])
    rope_d = din("rope", [NT, 128, 1792])
    ident_d = din("ident", [128, 128]); maskf_d = din("maskf", [128, 128]); maskb_d = din("maskb", [128, 128])
    fold_d = din("foldmask", [128, 8])
    z_d = nc.dram_tensor("z_spill", [NTT, 128, 256], BF16, kind="Internal").ap()
    gate_d = nc.dram_tensor("gate_spill", [2, 128, TT], BF16, kind="Internal").ap()
    if phase == "A":
        ktg_o = dout("KTg", [128, T], BF16); vg_o = dout("Vg", [T, 132], BF16)
        ktd_o = dout("KTd", [256, T], BF16); vd_o = dout("Vd", [T, 264], BF16)
        ssdex_o = dout("ssd_ex", [128, 520]); lruex_o = dout("lru_ex", [128, 8])
    else:
        ktg_o = nc.dram_tensor("KTg_loc", [128, T], BF16, kind="Internal").ap()
        vg_o = nc.dram_tensor("Vg_loc", [T, 132], BF16, kind="Internal").ap()
        ktd_o = nc.dram_tensor("KTd_loc", [256, T], BF16, kind="Internal").ap()
        vd_o = nc.dram_tensor("Vd_loc", [T, 264], BF16, kind="Internal").ap()
        ktg_a = din("KTg_all", [128, 4 * T], BF16); vg_a = din("Vg_all", [4 * T, 132], BF16)
        ktd_a = din("KTd_all", [256, 4 * T], BF16); vd_a = din("Vd_all", [4 * T, 264], BF16)
        ssdex_a = din("ssd_ex_all", [4, 128, 520]); lruex_a = din("lru_ex_all", [4, 128, 8])
        hout_d = dout("hT_out", [KC, 128, TT])
    dbg_out = {}
    if dbg:
        for nm, shp in dbg.items():
            dbg_out[nm] = dout("dbg_" + nm, shp)

    ARENA_BYTES = 206 * 1024
    with ExitStack() as es:
        arena_t = es.enter_context(nc.sbuf_tensor("arena", [128, ARENA_BYTES], U8))
        psb = [es.enter_context(nc.psum_tensor(f"ps{i}", [128, 512], F32)) for i in range(8)]
        csem = {e: es.enter_context(nc.semaphore("c_" + e)) for e in ENG}
        dsem = [es.enter_context(nc.semaphore(f"d{i}")) for i in range(NDSEM)]
        block = es.enter_context(nc.Block())
        P = Prog(nc)
        A = Arena(arena_t, ARENA_BYTES)
        ps_rr = [0]

        def psum(pool=None):
            if pool is None:
                i = ps_rr[0] % 8
                ps_rr[0] += 1
            else:
                lst, st = pool
                i = lst[st[0] % len(lst)]
                st[0] += 1
            return psb[i], ("ps", i)

        def dma(q, out, in_, r, w):
            return P.add(q, lambda e, out=out, in_=in_: e.dma_start(out=out, in_=in_), r=r, w=w, isdma=True)

        def mm(out, lhsT, rhs, start, stop, r, w, tp=None):
            if tp is None:
                return P.add("pe", lambda e: e.matmul(out, lhsT=lhsT, rhs=rhs, start=start, stop=stop), r=r, w=w)
            return P.add("pe", lambda e: e.matmul(out, lhsT=lhsT, rhs=rhs, start=start, stop=stop, tile_position=tp), r=r, w=w)

        def tr(out, in_, ident, r, w):
            return P.add("pe", lambda e: e.transpose(out, in_, ident), r=r, w=w)

        def act(out, in_, func, r, w, bias=None, scale=None, accum=None):
            kw = {}
            if bias is not None:
                kw["bias"] = bias
            if scale is not None:
                kw["scale"] = scale
            if accum is not None:
                kw["accum_out"] = accum
            return P.add("act", lambda e: e.activation(out=out, in_=in_, func=func, **kw), r=r, w=w)

        def tt(out, in0, in1, op, r, w, eng="dve"):
            return P.add(eng, lambda e: e.tensor_tensor(out=out, in0=in0, in1=in1, op=op), r=r, w=w)

        def ts(out, in0, s1, op0, r, w, s2=None, op1=None, eng="dve"):
            if op1 is None:
                return P.add(eng, lambda e: e.tensor_scalar(out=out, in0=in0, scalar1=s1, scalar2=None, op0=op0), r=r, w=w)
            return P.add(eng, lambda e: e.tensor_scalar(out=out, in0=in0, scalar1=s1, scalar2=s2, op0=op0, op1=op1), r=r, w=w)

        def stt(out, in0, scalar, in1, op0, op1, r, w):
            return P.add("dve", lambda e: e.scalar_tensor_tensor(out=out, in0=in0, scalar=scalar, in1=in1, op0=op0, op1=op1), r=r, w=w)

        def cp(out, in_, r, w, eng="dve"):
            if eng == "act":
                return P.add("act", lambda e: e.copy(out=out, in_=in_), r=r, w=w)
            return P.add(eng, lambda e: e.tensor_copy(out=out, in_=in_), r=r, w=w)

        def recip(out, in_, r, w):
            return P.add("dve", lambda e: e.reciprocal(out=out, in_=in_), r=r, w=w)

        def memset(ap, v, w, eng="dve"):
            return P.add(eng, lambda e: e.memset(ap, v), r=(), w=w)

        def scan(out, d0, d1, init, r, w):
            return P.add("dve", lambda e: e.tensor_tensor_scan(out=out, data0=d0, data1=d1, initial=init, op0=ALU.mult, op1=ALU.add), r=r, w=w)

        def red(out, in_, r, w):
            return P.add("dve", lambda e: e.tensor_reduce(out=out, in_=in_, axis=AX.X, op=ALU.add), r=r, w=w)

        def dump(name, ap, key):
            if name in dbg_out:
                dma("sp", dbg_out[name], ap, r=[key], w=[("dbg", name)])

        ident = A.alloc([128, 128], F32); identb = A.alloc([128, 128], BF16)
        ones = A.alloc([128, 128], F32)
        maskf = A.alloc([128, 128], F32); maskb = A.alloc([128, 128], F32)
        ones_row = A.alloc([128, 512], F32)
        dma("sp", ident, ident_d, r=[], w=["ident"])
        dma("sp", maskf, maskf_d, r=[], w=["maskf"]); dma("sp", maskb, maskb_d, r=[], w=["maskb"])
        cp(identb, ident, r=["ident"], w=["identb"])
        memset(ones, 1.0, w=["ones"]); memset(ones_row, 1.0, w=["ones_row"])
        masks = (maskf, maskb); mkeys = ("maskf", "maskb")

        def vec(name, dram, shape, dt=F32):
            v = A.alloc(shape, dt)
            dma("sp", v, dram, r=[], w=[name])
            return v
        hmask = vec("hmask", hmask_d, [128, 4]); cT = vec("cT", cT_d, [128, KC, 2]); bmod = vec("bmod", bmod_d, [128, 48])
        g1 = vec("g1", g1_d, [128, KC]); g2 = vec("g2", g2_d, [128, KC])
        cws = vec("cws", cws_d, [128, 6, 4]); cbs = vec("cbs", cbs_d, [128, 6])
        cwl = vec("cwl", cwl_d, [128, 2, 4]); cbl = vec("cbl", cbl_d, [128, 2])
        alog = vec("alog", alog_d, [8, 1]); dtb = vec("dtb", dtb_d, [8, 1])
        ssdd = vec("ssdd", ssdd_d, [128, 2, 256]); ssdng = vec("ssdng", ssdng_d, [128, 256])
        gqkg = vec("gqkg", gqkg_d, [128, 384]); dqkg = vec("dqkg", dqkg_d, [128, 512])
        lamv = vec("lamv", lamv_d, [128, 4, 32]); lami = vec("lami", lami_d, [128, 1]); subln = vec("subln", subln_d, [128, 64])
        lruw = vec("lruw", lruw_d, [128, 8, 128]); lrub = vec("lrub", lrub_d, [128, 8]); lrul = vec("lrul", lrul_d, [128, 4])
        fold = vec("fold", fold_d, [128, 8])
        mod = A.alloc([128, 48, 2], F32); gs1 = A.alloc([128, KC, 2], F32); gs2 = A.alloc([128, KC, 2], F32)
        cbf = A.alloc([128, KC, 2], BF16)
        a8 = A.alloc([8, 1], F32); c8sp = A.alloc([128, 4], F32)
        dsum = A.alloc([128, 256], F32)
        lam = A.alloc([128, 1], F32); nlam = A.alloc([128, 1], F32); sublns = A.alloc([128, 64], F32)
        KTc = A.alloc([128, 3, CTX], BF16)
        Vc = A.alloc([128, NCT, 396], BF16)
        hs_ctx = A.alloc([128, 2, 256], F32)
        um = A.alloc([128, KC, UW], BF16)

        m0 = A.mark()
        wmb = [A.alloc([128, KC, 128], BF16) for _ in range(3)]
        act(cbf, cT, AF.Silu, r=["cT"], w=["cbf"])
        for j in range(48):
            wb = wmb[j % 3]; wk = ("wmb", j % 3)
            dma("pool", wb, wmod_d[:, j * 128:(j + 1) * 128].rearrange("(kc p) c -> p kc c", p=128), r=[], w=[wk])
            ps, pk = psum()
            for kc in range(KC):
                mm(ps[:, 0:2], wb[:, kc, :], cbf[:, kc, :], kc == 0, kc == KC - 1, r=[wk, "cbf"], w=[pk])
            ts(mod[:, j, :], ps[:, 0:2], bmod[:, j:j + 1], ALU.add, r=[pk, "bmod"], w=["mod"])
        stt(gs1, mod[:, 8:16, :], 1.0, g1.unsqueeze(2).to_broadcast([128, KC, 2]), ALU.add, ALU.mult, r=["mod", "g1"], w=["gs1"])
        stt(gs2, mod[:, 32:40, :], 1.0, g2.unsqueeze(2).to_broadcast([128, KC, 2]), ALU.add, ALU.mult, r=["mod", "g2"], w=["gs2"])
        act(a8, alog, AF.Exp, r=["alog"], w=["a8"])
        ts(a8, a8, -1.0, ALU.mult, r=["a8"], w=["a8"])
        act(c8sp, lrul, AF.Exp, r=["lrul"], w=["c8sp"], scale=-1.0)
        act(c8sp, c8sp, AF.Ln, r=["c8sp"], w=["c8sp"], bias=1.0)
        ts(c8sp, c8sp, -8.0, ALU.mult, r=["c8sp"], w=["c8sp"])
        tt(dsum, ssdd[:, 0, :], ssdd[:, 1, :], ALU.add, r=["ssdd"], w=["dsum"])
        lt = A.alloc([128, 2, 32], F32); ls = A.alloc([128, 2], F32)
        tt(lt[:, 0, :], lamv[:, 0, :], lamv[:, 1, :], ALU.mult, r=["lamv"], w=["lt"])
        tt(lt[:, 1, :], lamv[:, 2, :], lamv[:, 3, :], ALU.mult, r=["lamv"], w=["lt"])
        red(ls, lt, r=["lt"], w=["ls"])
        act(ls, ls, AF.Exp, r=["ls"], w=["ls"])
        tt(lam, ls[:, 0:1], ls[:, 1:2], ALU.subtract, r=["ls"], w=["lam"])
        tt(lam, lam, lami, ALU.add, r=["lam", "lami"], w=["lam"])
        ts(nlam, lam, -1.0, ALU.mult, r=["lam"], w=["nlam"])
        omi = A.alloc([128, 1], F32)
        ts(omi, lami, -1.0, ALU.mult, r=["lami"], w=["omi"], s2=1.0, op1=ALU.add)
        ts(sublns, subln, omi[:, 0:1], ALU.mult, r=["subln", "omi"], w=["sublns"])
        P.barrier()
        A.release(m0)

        import os
        STOP = int(os.environ.get("KSTOP", "99"))

        def early():
            P.finish([("dbg", n) for n in dbg_out])
            P.emit(block, csem, dsem)
            return nc, dins, douts
        if STOP == 0:
            return early()
        sh1 = mod[:, 0:8, :]; gt1 = mod[:, 16:24, :]; sh2 = mod[:, 24:32, :]; gt2 = mod[:, 40:48, :]

        def norm_block(hblk, hk, bw, s, gs, gsk, sh, out, outk, scr):
            sq, rstd, tmp = scr
            ps, pk = psum()
            for ch in range(KC):
                q = sq[ch % 2]; qk = ("sq", ch % 2)
                act(q[:, :bw], hblk[:, ch, :bw], AF.Square, r=[hk], w=[qk])
                mm(ps[:, :bw], ones, q[:, :bw], ch == 0, ch == KC - 1, r=["ones", qk], w=[pk])
            act(rstd[:, :bw], ps[:, :bw], AF.Sqrt, r=[pk], w=["rstd"], bias=EPS, scale=1.0 / D)
            recip(rstd[:, :bw], rstd[:, :bw], r=["rstd"], w=["rstd"])
            for ch in range(KC):
                t = tmp[ch % 2]; tk = ("ntmp", ch % 2)
                tt(t[:, :bw], hblk[:, ch, :bw], rstd[:, :bw], ALU.mult, r=[hk, "rstd"], w=[tk])
                act(out[:, ch, :bw], t[:, :bw], AF.Identity, r=[tk, gsk, "mod"], w=[outk],
                    bias=sh[:, ch, s:s + 1], scale=gs[:, ch, s:s + 1])

        m_qt = A.mark()
        QT = A.alloc([128, 4, TT], BF16)
        m_mix = A.mark()
        xcT = A.alloc([128, 2, TT], F32)
        xs_tm = A.alloc([128, NTT, 256], F32)
        B_tm = A.alloc([128, NTT, 256], BF16)
        BT = A.alloc([128, 2, TT], BF16)
        CTt = A.alloc([128, 2, TT], BF16)
        Gtm = A.alloc([128, NTT, 24], F32)

        m1 = A.mark()
        hb = [A.alloc([128, KC, 512], F32) for _ in range(2)]
        nscr = ([A.alloc([128, 512], F32) for _ in range(2)], A.alloc([128, 512], F32),
                [A.alloc([128, 512], F32) for _ in range(2)])
        uh = A.alloc([128, KC, 4], BF16)
        bi = 0
        for (s, c0, ucol, n) in ((1, 0, 0, CTX), (0, CTX, LO, T)):
            for (o, bw) in blocks_of(n):
                h = hb[bi % 2]; hk = ("hb", bi % 2); bi += 1
                dma("sp", h[:, :, :bw], hT_d[:, :, c0 + o:c0 + o + bw].rearrange("k p t -> p k t"), r=[], w=[hk])
                norm_block(h, hk, bw, s, gs1, "gs1", sh1, um[:, :, ucol + o:ucol + o + bw], "uT", nscr)
        h = hb[bi % 2]; hk = ("hb", bi % 2); bi += 1
        dma("sp", h[:, :, :4], halo_d.rearrange("k p t -> p k t"), r=[], w=[hk])
        norm_block(h, hk, 4, 0, gs1, "gs1", sh1, uh, "uh", nscr)
        tt(uh, uh, hmask.unsqueeze(1).to_broadcast([128, KC, 4]), ALU.mult, r=["uh", "hmask"], w=["uh"])
        cp(um[:, :, CTX:CTX + 2], uh[:, :, 0:2], r=["uh"], w=["uT"])
        cp(um[:, :, LO + T:LO + T + 1], uh[:, :, 2:3], r=["uh"], w=["uT"])
        dump("uT", um[:, :, 0:UW], "uT")
        P.barrier()
        A.release(m1)
        if STOP == 1:
            return early()

        m2 = A.mark()
        PW = CTX + 3 + T + 3
        pre = [A.alloc([128, PW], F32) for _ in range(2)]
        wcb = [A.alloc([128, KC, 128], BF16) for _ in range(2)]
        cvo = [A.alloc([128, TT], F32) for _ in range(2)]
        dtT = pre[0][0:8, 0:TT]; dtaT = pre[1][0:8, 0:TT]; GfT = cvo[0][0:8, 0:TT]; GbT = cvo[1][0:8, 0:TT]
        dte = A.alloc([8, 512], F32)
        ones8 = A.alloc([8, TT], F32)
        memset(ones8, 1.0, w=["ones8"])
        gtb = [A.alloc([128, 512], BF16) for _ in range(2)]
        streams = ((1, 0, CTX, 0), (0, CTX, T, CTX))
        chunks = []
        for i in range(6):
            chunks.append(("xbc", i, 256 + 128 * i, 128))
        for i in range(2):
            chunks.append(("xr", i, 1800 + 128 * i, 128))
        for i in range(2):
            chunks.append(("gate", i, 1544 + 128 * i, 128))
        chunks.append(("dt", 0, 1024, 8))
        for ci, (kind, i, wc0, cw) in enumerate(chunks):
            wb = wcb[ci % 2]; wk = ("wcb", ci % 2)
            dma("pool", wb[:, :, :cw], win_d[:, wc0:wc0 + cw].rearrange("(kc p) c -> p kc c", p=128), r=[], w=[wk])
            if kind in ("xbc", "xr"):
                pr = pre[ci % 2]; prk = ("pre", ci % 2)
                memset(pr[:, 0:2], 0.0, w=[prk]); memset(pr[:, CTX + 2:CTX + 3], 0.0, w=[prk])
                blks = [(o, 2 + o, bw) for (o, bw) in blocks_of(CTX)] + \
                       [(CTX + o, CTX + 3 + o, bw) for (o, bw) in blocks_of(T + 3)]
                for (uc, pc, bw) in blks:
                    ps, pk = psum()
                    for kc in range(KC):
                        mm(ps[:cw, :bw], wb[:, kc, :cw], um[:, kc, uc:uc + bw], kc == 0, kc == KC - 1, r=[wk, "uT"], w=[pk])
                    cp(pr[:, pc:pc + bw], ps[:cw, :bw], r=[pk], w=[prk], eng="act")
                cw_t, cb_t = (cws[:, i, :], cbs[:, i:i + 1]) if kind == "xbc" else (cwl[:, i, :], cbl[:, i:i + 1])
                o_ = cvo[ci % 2]; ok = ("cvo", ci % 2)
                for (s, c0, n, _) in streams:
                    base = 0 if s == 1 else CTX + 3
                    act(o_[:, c0:c0 + n], pr[:, base:base + n], AF.Identity, r=[prk, "cws", "cwl", "cbs", "cbl"], w=[ok],
                        bias=cb_t, scale=cw_t[:, 0:1])
                    for j in range(1, 4):
                        stt(o_[:, c0:c0 + n], pr[:, base + j:base + j + n], cw_t[:, j:j + 1], o_[:, c0:c0 + n],
                            ALU.mult, ALU.add, r=[prk, ok], w=[ok])
                if kind == "xr":
                    cp(xcT[:, i, :], o_[:, :], r=[ok], w=["xcT"], eng="act")
                elif i < 2:
                    xt_ = o_; xk = ok
                    act(xt_, o_, AF.Silu, r=[ok], w=[xk])
                    for t in range(NTT):
                        ps, pk = psum()
                        tr(ps[:, 0:128], xt_[:, t * 128:(t + 1) * 128], ident, r=[xk, "ident"], w=[pk])
                        cp(xs_tm[:, t, i * 128:(i + 1) * 128], ps[:, 0:128], r=[pk], w=["xs_tm"])
                elif i < 4:
                    g = i - 2
                    act(BT[:, g, :], o_, AF.Silu, r=[ok], w=["BT"])
                    for t in range(NTT):
                        ps, pk = psum()
                        pb = ps.bitcast(BF16)
                        tr(pb[:, 0:128], BT[:, g, t * 128:(t + 1) * 128], identb, r=["BT", "identb"], w=[pk])
                        cp(B_tm[:, t, g * 128:(g + 1) * 128], pb[:, 0:128], r=[pk], w=["B_tm"])
                else:
                    act(CTt[:, i - 4, :], o_, AF.Silu, r=[ok], w=["CT"])
            elif kind == "gate":
                for (s, c0, n, u0) in streams:
                    ub = u0 if s == 1 else LO
                    for bi2, (o, bw) in enumerate(blocks_of(n)):
                        ps, pk = psum()
                        for kc in range(KC):
                            mm(ps[:, :bw], wb[:, kc, :], um[:, kc, ub + o:ub + o + bw], kc == 0, kc == KC - 1, r=[wk, "uT"], w=[pk])
                        gb_ = gtb[bi2 % 2]; gk = ("gtb", bi2 % 2)
                        act(gb_[:, :bw], ps[:, :bw], AF.Gelu_apprx_tanh, r=[pk], w=[gk])
                        dma("sp", gate_d[i, :, c0 + o:c0 + o + bw], gb_[:, :bw], r=[gk], w=["gate_d"])
            else:
                P.barrier()
                for (s, c0, n, u0) in streams:
                    ub = u0 if s == 1 else LO
                    for (o, bw) in blocks_of(n):
                        ps, pk = psum()
                        for kc in range(KC):
                            mm(ps[:8, :bw], wb[:, kc, :8], um[:, kc, ub + o:ub + o + bw], kc == 0, kc == KC - 1, r=[wk, "uT"], w=[pk])
                        act(dte[:, :bw], ps[:8, :bw], AF.Exp, r=[pk, "dtb"], w=["dte"], bias=dtb[:, 0:1])
                        act(dtT[:, c0 + o:c0 + o + bw], dte[:, :bw], AF.Ln, r=["dte"], w=["dtT"], bias=1.0)
                ts(dtaT, dtT, a8[:, 0:1], ALU.mult, r=["dtT", "a8"], w=["dtaT"])
                for (s, c0, n, _) in streams:
                    scan(GfT[:, c0:c0 + n], ones8[:, :n], dtaT[:, c0:c0 + n], 0.0, r=["dtaT", "ones8"], w=["GfT"])
                    scan(GbT[:, c0:c0 + n][:, ::-1], ones8[:, :n], dtaT[:, c0:c0 + n][:, ::-1], 0.0, r=["dtaT", "ones8"], w=["GbT"])
                for t in range(NTT):
                    ps, pk = psum()
                    tr(ps[:, 0:8], dtT[:, t * 128:(t + 1) * 128], ident[:8, :8], r=["dtT", "ident"], w=[pk])
                    tr(ps[:, 8:16], GfT[:, t * 128:(t + 1) * 128], ident[:8, :8], r=["GfT", "ident"], w=[pk])
                    tr(ps[:, 16:24], GbT[:, t * 128:(t + 1) * 128], ident[:8, :8], r=["GbT", "ident"], w=[pk])
                    cp(Gtm[:, t, :], ps[:, 0:24], r=[pk], w=["Gtm"])
        dump("xcT", xcT, "xcT"); dump("xs_tm", xs_tm, "xs_tm"); dump("Gtm", Gtm, "Gtm")
        P.barrier()
        A.release(m2)
        if STOP == 2:
            return early()

        m3 = A.mark()
        Wtm = A.alloc([128, KC, 1536], BF16)
        for (d0, s0, n) in ((0, 0, 256), (256, 1032, 512), (768, 2056, 768)):
            for kc in range(KC):
                dma("pool", Wtm[:, kc, d0:d0 + n], win_d[kc * 128:(kc + 1) * 128, s0:s0 + n], r=[], w=["Wtm"])
        ropeb = [A.alloc([128, 1792], F32) for _ in range(2)]
        x6 = [A.alloc([128, 384], F32) for _ in range(2)]; x16 = [A.alloc([128, 512], F32) for _ in range(2)]
        sqs = A.alloc([128, 512], F32); ssb = A.alloc([128, 16], F32)
        t1 = A.alloc([128, 512], F32); t2 = A.alloc([128, 512], F32)
        qo = [A.alloc([128, 896], BF16) for _ in range(2)]
        zt = [A.alloc([128, 256], BF16) for _ in range(2)]
        vst = [A.alloc([128, 396], BF16) for _ in range(2)]
        KTs = [A.alloc([128, 3, 128], BF16) for _ in range(2)]
        for b_ in range(2):
            memset(vst[b_], 1.0, w=[("vst", b_)])
        memset(Vc, 1.0, w=["Vc"])

        def qknorm(x, xk, nh, hd, g_bc, gk, rope_c, rope_s, rk, out, outk):
            n = nh * hd
            act(sqs[:, :n], x[:, :n], AF.Square, r=[xk], w=["sqs"])
            red(ssb[:, :nh], sqs[:, :n].rearrange("p (a b) -> p a b", a=nh), r=["sqs"], w=["ssb"])
            act(ssb[:, :nh], ssb[:, :nh], AF.Sqrt, r=["ssb"], w=["ssb"], bias=EPS, scale=1.0 / hd)
            recip(ssb[:, :nh], ssb[:, :nh], r=["ssb"], w=["ssb"])
            x3 = x[:, :n].rearrange("p (a b) -> p a b", a=nh)
            tt(x3, x3, ssb[:, :nh].unsqueeze(2).to_broadcast([128, nh, hd]), ALU.mult, r=[xk, "ssb"], w=[xk])
            if rope_c is None:
                tt(out, x[:, :n], g_bc, ALU.mult, r=[xk, gk], w=[outk])
                return
            tt(x[:, :n], x[:, :n], g_bc, ALU.mult, r=[xk, gk], w=[xk])
            tt(t1[:, :n], x[:, :n], rope_c, ALU.mult, r=[xk, rk], w=["t1"])
            i_ = hd // 4
            xsw = x[:, :n].rearrange("p (a w i) -> p a w i", w=2, i=i_)[:, :, ::-1, :]
            tt(t2[:, :n].rearrange("p (a w i) -> p a w i", w=2, i=i_), xsw,
               rope_s.rearrange("p (a w i) -> p a w i", w=2, i=i_), ALU.mult, r=[xk, rk], w=["t2"])
            tt(out, t1[:, :n], t2[:, :n], ALU.add, r=["t1", "t2"], w=[outk])

        KSUB = float(os.environ.get("KSUB", "99"))
        for t in range(NTT if KSUB > 0 else 0):
            isctx = t < NCT
            ucol = t * 128 if isctx else LO + (t - NCT) * 128
            b2 = t % 2
            pss = []
            for jb in range(3):
                ps, pk = psum()
                for kc in range(KC):
                    mm(ps[:, :], um[:, kc, ucol:ucol + 128], Wtm[:, kc, jb * 512:(jb + 1) * 512], kc == 0, kc == KC - 1,
                       r=["uT", "Wtm"], w=[pk])
                pss.append((ps, pk))
            (p0, k0), (p1, k1), (p2, k2) = pss
            act(zt[b2], p0[:, 0:256], AF.Silu, r=[k0], w=[("zt", b2)])
            dma("sp", z_d[t], zt[b2], r=[("zt", b2)], w=["z_d"])
            if KSUB <= 1:
                continue
            xa = x6[b2]; xak = ("x6", b2); xb = x16[b2]; xbk = ("x16", b2)
            cp(xa[:, 0:256], p0[:, 256:512], r=[k0], w=[xak], eng="act")
            cp(xa[:, 256:384], p1[:, 0:128], r=[k1], w=[xak], eng="act")
            cp(xb[:, 0:256], p1[:, 256:512], r=[k1], w=[xbk], eng="act")
            cp(xb[:, 256:512], p2[:, 0:256], r=[k2], w=[xbk], eng="act")
            if KSUB <= 1.3:
                continue
            if isctx:
                for hh_ in range(2):
                    cp(Vc[:, t, hh_ * 66:hh_ * 66 + 64], p1[:, 128 + hh_ * 64:192 + hh_ * 64], r=[k1], w=["Vc"], eng="act")
                for hh_ in range(4):
                    cp(Vc[:, t, 132 + hh_ * 66:132 + hh_ * 66 + 64], p2[:, 256 + hh_ * 64:320 + hh_ * 64], r=[k2], w=["Vc"], eng="act")
            else:
                v_ = vst[b2]; vk = ("vst", b2)
                for hh_ in range(2):
                    cp(v_[:, hh_ * 66:hh_ * 66 + 64], p1[:, 128 + hh_ * 64:192 + hh_ * 64], r=[k1], w=[vk], eng="act")
                for hh_ in range(4):
                    cp(v_[:, 132 + hh_ * 66:132 + hh_ * 66 + 64], p2[:, 256 + hh_ * 64:320 + hh_ * 64], r=[k2], w=[vk], eng="act")
                tl = t - NCT
                if KSUB <= 1.6:
                    continue
                dma("sp", vg_o[tl * 128:(tl + 1) * 128, :], v_[:, 0:132], r=[vk], w=["vg_o"])
                dma("sp", vd_o[tl * 128:(tl + 1) * 128, :], v_[:, 132:396], r=[vk], w=["vd_o"])
            if KSUB <= 2:
                continue
            q_ = qo[b2]; qk_ = ("qo", b2)
            if isctx:
                qknorm(xa, xak, 6, 64, gqkg, "gqkg", None, None, None, q_[:, 0:384], qk_)
                qknorm(xb, xbk, 16, 32, dqkg, "dqkg", None, None, None, q_[:, 384:896], qk_)
            else:
                rb = ropeb[b2]; rk = ("ropeb", b2)
                dma("sp", rb, rope_d[t - NCT], r=[], w=[rk])
                qknorm(xa, xak, 6, 64, gqkg, "gqkg", rb[:, 0:384], rb[:, 384:768], rk, q_[:, 0:384], qk_)
                qknorm(xb, xbk, 16, 32, dqkg, "dqkg", rb[:, 768:1280], rb[:, 1280:1792], rk, q_[:, 384:896], qk_)
            if KSUB <= 3:
                continue
            tc_ = t * 128
            ps, pk = psum(); pb = ps.bitcast(BF16)
            srcs = [(0, 0), (128, 1), (384, 2), (512, 3)]
            for j, (c0, c) in enumerate(srcs):
                tr(pb[:, j * 128:(j + 1) * 128], q_[:, c0:c0 + 128], identb, r=[qk_, "identb"], w=[pk])
            cp(QT[:, :, tc_:tc_ + 128], pb[:, 0:512].rearrange("p (c t) -> p c t", c=4), r=[pk], w=["QT"])
            ps, pk = psum(); pb = ps.bitcast(BF16)
            for j, c0 in enumerate((256, 640, 768)):
                tr(pb[:, j * 128:(j + 1) * 128], q_[:, c0:c0 + 128], identb, r=[qk_, "identb"], w=[pk])
            if isctx:
                cp(KTc[:, :, tc_:tc_ + 128], pb[:, 0:384].rearrange("p (c t) -> p c t", c=3), r=[pk], w=["KTc"], eng="act")
            else:
                tl = (t - NCT) * 128
                ks_ = KTs[b2]; ksk = ("KTs", b2)
                cp(ks_, pb[:, 0:384].rearrange("p (c t) -> p c t", c=3), r=[pk], w=[ksk], eng="act")
                dma("sp", ktg_o[:, tl:tl + 128], ks_[:, 0, :], r=[ksk], w=["ktg_o"])
                dma("sp", ktd_o[0:128, tl:tl + 128], ks_[:, 1, :], r=[ksk], w=["ktd_o"])
                dma("sp", ktd_o[128:256, tl:tl + 128], ks_[:, 2, :], r=[ksk], w=["ktd_o"])
        dump("QT", QT, "QT")
        P.barrier()
        A.release(m3)
        if STOP == 3:
            return early()

        def lru_run(final):
            mm_ = A.mark()
            rb_ = [A.alloc([128, 512], F32) for _ in range(2)]; ib_ = [A.alloc([128, 512], F32) for _ in range(2)]
            ab_ = [A.alloc([128, 512], F32) for _ in range(2)]; ub_ = [A.alloc([128, 512], F32) for _ in range(2)]
            sb_ = [A.alloc([128, 512], F32) for _ in range(2)]; hbf = [A.alloc([128, 512], F32) for _ in range(2)]
            hf = A.alloc([128, TT], F32)
            rsp = A.alloc([128, 16], F32); Ls = A.alloc([128, 4], F32)
            hinit = A.alloc([128, 4], F32); lex = A.alloc([128, 8], F32)
            gld = [A.alloc([128, 512], BF16) for _ in range(2)]
            if final:
                lexa = A.alloc([128, 4, 8], F32)
                dma("sp", lexa, lruex_a.rearrange("j p c -> p j c"), r=[], w=["lexa"])
                ftmp = A.alloc([128, 8], F32)
            cnt = 0
            for c in range(2):
                for dr in range(2):
                    col = dr * 2 + c
                    wr = lruw[:, dr * 4 + 0 * 2 + c, :]; wi = lruw[:, dr * 4 + 1 * 2 + c, :]
                    br = lrub[:, dr * 4 + c:dr * 4 + c + 1]; bi_ = lrub[:, dr * 4 + 2 + c:dr * 4 + 2 + c + 1]
                    cs_ = c8sp[:, dr * 2 + c:dr * 2 + c + 1]
                    for (s, c0, n, _) in ((1, 0, CTX, 0), (0, CTX, T, 0)):
                        if s == 1:
                            memset(hinit[:, col:col + 1], 0.0, w=["hinit"])
                        elif final:
                            order = range(4) if dr == 0 else range(3, -1, -1)
                            for j in order:
                                mk = fold[:, dr * 4 + j:dr * 4 + j + 1]
                                ts(ftmp[:, 0:1], lexa[:, j, 4 + col:4 + col + 1], mk, ALU.mult, r=["lexa", "fold"], w=["ftmp"])
                                act(ftmp[:, 0:1], ftmp[:, 0:1], AF.Exp, r=["ftmp"], w=["ftmp"])
                                ts(ftmp[:, 1:2], lexa[:, j, col:col + 1], mk, ALU.mult, r=["lexa", "fold"], w=["ftmp"])
                                stt(hinit[:, col:col + 1], hinit[:, col:col + 1], ftmp[:, 0:1], ftmp[:, 1:2], ALU.mult, ALU.add,
                                    r=["hinit", "ftmp"], w=["hinit"])
                        else:
                            memset(hinit[:, col:col + 1], 0.0, w=["hinit"])
                        blks = blocks_of(n)
                        if dr == 1:
                            blks = blks[::-1]
                        hprev = hinit[:, col:col + 1]; hpk = "hinit"
                        nb = 0
                        for (o, bw) in blks:
                            k = cnt % 2; cnt += 1
                            xc = xcT[:, c, c0 + o:c0 + o + bw]
                            ps, pk = psum()
                            mm(ps[:, :bw], wr, xc, True, True, r=["lruw", "xcT"], w=[pk])
                            ps2, pk2 = psum()
                            mm(ps2[:, :bw], wi, xc, True, True, r=["lruw", "xcT"], w=[pk2])
                            act(rb_[k][:, :bw], ps[:, :bw], AF.Sigmoid, r=[pk, "lrub"], w=[("rb", k), "rsp"], bias=br, accum=rsp[:, nb:nb + 1])
                            act(ib_[k][:, :bw], ps2[:, :bw], AF.Sigmoid, r=[pk2, "lrub"], w=[("ib", k)], bias=bi_)
                            act(ab_[k][:, :bw], rb_[k][:, :bw], AF.Exp, r=[("rb", k), "c8sp"], w=[("ab", k)], scale=cs_)
                            tt(sb_[k][:, :bw], ab_[k][:, :bw], ab_[k][:, :bw], ALU.mult, r=[("ab", k)], w=[("sb", k)])
                            act(sb_[k][:, :bw], sb_[k][:, :bw], AF.Sqrt, r=[("sb", k)], w=[("sb", k)], bias=1.0, scale=-1.0)
                            tt(ub_[k][:, :bw], ib_[k][:, :bw], xc, ALU.mult, r=[("ib", k), "xcT"], w=[("ub", k)])
                            tt(ub_[k][:, :bw], ub_[k][:, :bw], sb_[k][:, :bw], ALU.mult, r=[("ub", k), ("sb", k)], w=[("ub", k)])
                            if dr == 0:
                                dst = hf[:, c0 + o:c0 + o + bw]; dk = "hf"
                                scan(dst, ab_[k][:, :bw], ub_[k][:, :bw], hprev, r=[("ab", k), ("ub", k), hpk], w=[dk])
                                hprev = hf[:, c0 + o + bw - 1:c0 + o + bw]; hpk = dk
                            else:
                                dst = hbf[k][:, :bw]; dk = ("hbf", k)
                                scan(dst[:, ::-1], ab_[k][:, :bw][:, ::-1], ub_[k][:, :bw][:, ::-1], hprev,
                                     r=[("ab", k), ("ub", k), hpk], w=[dk])
                                hprev = hbf[k][:, 0:1]; hpk = dk
                                if final:
                                    g_ = gld[k]; gk = ("gld", k)
                                    dma("sp", g_[:, :bw], gate_d[c, :, c0 + o:c0 + o + bw], r=["gate_d"], w=[gk])
                                    tt(sb_[k][:, :bw], dst, hf[:, c0 + o:c0 + o + bw], ALU.add, r=[dk, "hf"], w=[("sb", k)])
                                    ucol = (0 if s == 1 else CTX) + o
                                    tt(um[:, 4 + c, ucol:ucol + bw], sb_[k][:, :bw], g_[:, :bw], ALU.mult, r=[("sb", k), gk], w=["mixT"])
                            nb += 1
                        if s == 1:
                            cp(hinit[:, col:col + 1], hprev, r=[hpk], w=["hinit"])
                        elif not final:
                            cp(lex[:, col:col + 1], hprev, r=[hpk], w=["lex"])
                            red(Ls[:, col:col + 1], rsp[:, 0:nb], r=["rsp"], w=["Ls"])
                            tt(lex[:, 4 + col:5 + col], Ls[:, col:col + 1], cs_, ALU.mult, r=["Ls", "c8sp"], w=["lex"])
            if not final:
                dma("sp", lruex_o, lex, r=["lex"], w=["lruex_o"])
            P.barrier()
            A.release(mm_)

        def ssd_run(final):
            mm_ = A.mark()
            hS = A.alloc([128, 256], F32); hSb = A.alloc([128, 256], BF16)
            ge = [A.alloc([128, 4], F32) for _ in range(2)]; ngp = A.alloc([128, 4], F32)
            te = A.alloc([128, 4], F32); wv = A.alloc([128, 4], F32); cd = A.alloc([128, 4], F32)
            xdw = [A.alloc([128, 256], BF16) for _ in range(2)]; xdt = [A.alloc([128, 256], BF16) for _ in range(2)]
            sx = A.alloc([128, 520], F32)
            if final:
                yacc = A.alloc([128, NTT, 256], F32)
                CBm = A.alloc([128, 256], F32); d4 = A.alloc([128, 512], F32); SD4 = A.alloc([128, 512], BF16)
                ec4 = A.alloc([128, 512], F32); Ct4 = A.alloc([128, 512], BF16)
                sxa = A.alloc([128, 4, 520], F32)
                dma("sp", sxa, ssdex_a.rearrange("j p c -> p j c"), r=[], w=["sxa"])
                fD = A.alloc([128, 4], F32); fE = A.alloc([128, 256], F32)
                zl = [A.alloc([128, 256], BF16) for _ in range(2)]
                yb_ = A.alloc([128, 256], F32); ysq = A.alloc([128, 256], F32); yss = A.alloc([128, 2], F32)
                ymb = A.alloc([128, 256], BF16)
            for dr in range(2):
                for (s, t0, nt) in ((1, 0, NCT), (0, NCT, NT)):
                    if s == 1 or not final:
                        memset(hS, 0.0, w=["hS"])
                    else:
                        cp(hS, hs_ctx[:, dr, :], r=["hs_ctx"], w=["hS"])
                        order = range(4) if dr == 0 else range(3, -1, -1)
                        for j in order:
                            mk = fold[:, dr * 4 + j:dr * 4 + j + 1]
                            ts(fD, sxa[:, j, 512 + dr * 4:516 + dr * 4], mk, ALU.mult, r=["sxa", "fold"], w=["fD"])
                            act(fD, fD, AF.Exp, r=["fD"], w=["fD"])
                            ts(fE, sxa[:, j, dr * 256:(dr + 1) * 256], mk, ALU.mult, r=["sxa", "fold"], w=["fE"])
                            for h in range(4):
                                stt(hS[:, h * 64:(h + 1) * 64], hS[:, h * 64:(h + 1) * 64], fD[:, h:h + 1], fE[:, h * 64:(h + 1) * 64],
                                    ALU.mult, ALU.add, r=["hS", "fD", "fE"], w=["hS"])
                    if final:
                        cp(hSb, hS, r=["hS"], w=["hSb"], eng="act")
                    memset(ngp, 0.0, w=["ngp"])
                    tiles = list(range(t0, t0 + nt))
                    if dr == 1:
                        tiles = tiles[::-1]
                    gp = None
                    for it, t in enumerate(tiles):
                        k = it % 2
                        tc_ = t * 128
                        gcol = 8 if dr == 0 else 20
                        psG, pkG = psum()
                        for h in range(4):
                            mm(psG[:, h * 128:(h + 1) * 128], Gtm[:, t, gcol + h:gcol + h + 1].to_broadcast([128, 128]), ident,
                               True, True, r=["Gtm", "ident"], w=[pkG])
                        ecol = 127 if dr == 0 else 0
                        g_ = ge[k]; gk_ = ("ge", k)
                        cp(g_, psG.rearrange("p (h l) -> p h l", h=4)[:, :, ecol], r=[pkG], w=[gk_], eng="act")
                        tt(te, g_, Gtm[:, t, gcol:gcol + 4], ALU.subtract, r=[gk_, "Gtm"], w=["te"])
                        act(te, te, AF.Exp, r=["te"], w=["te"])
                        tt(wv, te, Gtm[:, t, dr * 4:dr * 4 + 4], ALU.mult, r=["te", "Gtm"], w=["wv"])
                        tt(cd, g_, ngp, ALU.add, r=[gk_, "ngp"], w=["cd"])
                        act(cd, cd, AF.Exp, r=["cd"], w=["cd"])
                        xs3 = xs_tm[:, t, :].rearrange("p (h c) -> p h c", h=4)
                        tt(xdw[k].rearrange("p (h c) -> p h c", h=4), xs3, wv.unsqueeze(2).to_broadcast([128, 4, 64]), ALU.mult,
                           r=["xs_tm", "wv"], w=[("xdw", k)])
                        psS, pkS = psum()
                        for g in range(2):
                            mm(psS[:, g * 128:(g + 1) * 128], B_tm[:, t, g * 128:(g + 1) * 128], xdw[k][:, g * 128:(g + 1) * 128],
                               True, True, r=["B_tm", ("xdw", k)], w=[pkS])
                        if final:
                            psC, pkC = psum()
                            for g in range(2):
                                mm(psC[:, g * 128:(g + 1) * 128], BT[:, g, tc_:tc_ + 128], CTt[:, g, tc_:tc_ + 128], True, True,
                                   r=["BT", "CT"], w=[pkC])
                            tt(CBm.rearrange("p (g l) -> p g l", g=2), psC[:, 0:256].rearrange("p (g l) -> p g l", g=2),
                               masks[dr].unsqueeze(1).to_broadcast([128, 2, 128]), ALU.mult, r=[pkC, mkeys[dr]], w=["CBm"])
                            for h in range(4):
                                ts(d4[:, h * 128:(h + 1) * 128], psG[:, h * 128:(h + 1) * 128], Gtm[:, t, gcol + h:gcol + h + 1], ALU.subtract,
                                   r=[pkG, "Gtm"], w=["d4"], s2=0.0, op1=ALU.min)
                            act(d4, d4, AF.Exp, r=["d4"], w=["d4"])
                            tt(SD4.rearrange("p (g a l) -> p g a l", g=2, a=2), d4.rearrange("p (g a l) -> p g a l", g=2, a=2),
                               CBm.rearrange("p (g l) -> p g l", g=2).unsqueeze(2).to_broadcast([128, 2, 2, 128]), ALU.mult,
                               r=["d4", "CBm"], w=["SD4"])
                            tt(xdt[k].rearrange("p (h c) -> p h c", h=4), xs3, Gtm[:, t, dr * 4:dr * 4 + 4].unsqueeze(2).to_broadcast([128, 4, 64]),
                               ALU.mult, r=["xs_tm", "Gtm"], w=[("xdt", k)])
                            for h in range(4):
                                act(ec4[:, h * 128:(h + 1) * 128], psG[:, h * 128:(h + 1) * 128], AF.Exp, r=[pkG, "ngp"], w=["ec4"],
                                    bias=ngp[:, h:h + 1])
                            tt(Ct4.rearrange("p (g a l) -> p g a l", g=2, a=2), ec4.rearrange("p (g a l) -> p g a l", g=2, a=2),
                               CTt[:, :, tc_:tc_ + 128].unsqueeze(2).to_broadcast([128, 2, 2, 128]), ALU.mult, r=["ec4", "CT"], w=["Ct4"])
                            psY, pkY = psum()
                            for h in range(4):
                                mm(psY[:, h * 64:(h + 1) * 64], SD4[:, h * 128:(h + 1) * 128], xdt[k][:, h * 64:(h + 1) * 64], True, False,
                                   r=["SD4", ("xdt", k)], w=[pkY])
                                mm(psY[:, h * 64:(h + 1) * 64], Ct4[:, h * 128:(h + 1) * 128], hSb[:, h * 64:(h + 1) * 64], False, True,
                                   r=["Ct4", "hSb"], w=[pkY])
                            if dr == 0:
                                cp(yacc[:, t, :], psY[:, 0:256], r=[pkY], w=[("yacc", t)], eng="act")
                            else:
                                tt(yacc[:, t, :], yacc[:, t, :], psY[:, 0:256], ALU.add, r=[pkY, ("yacc", t)], w=[("yacc", t)])
                        for h in range(4):
                            stt(hS[:, h * 64:(h + 1) * 64], hS[:, h * 64:(h + 1) * 64], cd[:, h:h + 1], psS[:, h * 64:(h + 1) * 64],
                                ALU.mult, ALU.add, r=["hS", "cd", pkS, "hSb"], w=["hS"])
                        if final:
                            cp(hSb, hS, r=["hS", "Ct4"], w=["hSb"], eng="act")
                        ts(ngp, g_, -1.0, ALU.mult, r=[gk_, "ec4", "cd"], w=["ngp"])
                    if s == 1:
                        cp(hs_ctx[:, dr, :], hS, r=["hS"], w=["hs_ctx"])
                    elif not final:
                        cp(sx[:, dr * 256:(dr + 1) * 256], hS, r=["hS"], w=["sx"])
                        ts(sx[:, 512 + dr * 4:516 + dr * 4], ngp, -1.0, ALU.mult, r=["ngp"], w=["sx"])
            if not final:
                dma("sp", ssdex_o, sx, r=["sx"], w=["ssdex_o"])
            else:
                for t in range(NTT):
                    k = t % 2
                    dma("sp", zl[k], z_d[t], r=["z_d"], w=[("zl", k)])
                    tt(yb_, xs_tm[:, t, :], dsum, ALU.mult, r=["xs_tm", "dsum"], w=["yb"])
                    tt(yb_, yb_, yacc[:, t, :], ALU.add, r=["yb", ("yacc", t)], w=["yb"])
                    tt(yb_, yb_, zl[k], ALU.mult, r=["yb", ("zl", k)], w=["yb"])
                    act(ysq, yb_, AF.Square, r=["yb"], w=["ysq"])
                    red(yss, ysq.rearrange("p (g c) -> p g c", g=2), r=["ysq"], w=["yss"])
                    act(yss, yss, AF.Sqrt, r=["yss"], w=["yss"], bias=EPS, scale=1.0 / 128)
                    recip(yss, yss, r=["yss"], w=["yss"])
                    tt(yb_.rearrange("p (g c) -> p g c", g=2), yb_.rearrange("p (g c) -> p g c", g=2),
                       yss.unsqueeze(2).to_broadcast([128, 2, 128]), ALU.mult, r=["yb", "yss"], w=["yb"])
                    tt(ymb, yb_, ssdng, ALU.mult, r=["yb", "ssdng"], w=["ymb"])
                    ps, pk = psum(); pb = ps.bitcast(BF16)
                    for g in range(2):
                        tr(pb[:, g * 128:(g + 1) * 128], ymb[:, g * 128:(g + 1) * 128], identb, r=["ymb", "identb"], w=[pk])
                    cp(um[:, 0:2, t * 128:(t + 1) * 128], pb[:, 0:256].rearrange("p (c t) -> p c t", c=2), r=[pk], w=["mixT"], eng="act")
            P.barrier()
            A.release(mm_)

        if phase == "A":
            lru_run(False)
            ssd_run(False)
            fin = P.finish(["ktg_o", "ktd_o", "vg_o", "vd_o", "ssdex_o", "lruex_o"] + [("dbg", n) for n in dbg_out])
            P.emit(block, csem, dsem)
            return nc, dins, douts

        lru_run(True)
        ssd_run(True)
        A.release(m_mix)
        dump("mixT_a", um[:, :, 0:TT], "mixT")

        m8 = A.mark()
        mix_tm = A.alloc([128, NTT, 512], BF16)
        KTb = [A.alloc([128, LK], BF16) for _ in range(2)]
        Vb = [A.alloc([128, NK, 132], BF16) for _ in range(2)]
        PTb = [A.alloc([128, 512], BF16) for _ in range(4)]
        OTb = [A.alloc([66, 512], F32) for _ in range(2)]
        o1 = A.alloc([128, NTT, 64], F32)
        rinv = A.alloc([128, 1], F32); otmp = A.alloc([128, 64], F32); osq = A.alloc([128, 64], F32); oss = A.alloc([128, 1], F32)
        poolO = ([0, 1], [0]); poolS = ([2, 3, 4, 5], [0]); poolT = ([6, 7], [0])
        units = [("g", 0), ("g", 1), ("d", 0), ("d", 1)]
        ptc = 0; otc = 0
        for ui, (kind, c) in enumerate(units):
            kb = KTb[ui % 2]; kk = ("KTb", ui % 2); vb = Vb[ui % 2]; vk = ("Vb", ui % 2)
            if kind == "g":
                for half in range(2):
                    cp(kb[half * 64:(half + 1) * 64, 0:CTX], KTc[c * 64:(c + 1) * 64, 0, :], r=["KTc"], w=[kk])
                    dma("sp", kb[half * 64:(half + 1) * 64, CTX:LK], ktg_a[c * 64:(c + 1) * 64, :], r=[], w=[kk])
                cp(vb[:, 0:NCT, 0:66], Vc[:, :, c * 66:(c + 1) * 66], r=["Vc"], w=[vk])
                for v0 in range(0, NK - NCT, 16):
                    v1 = min(v0 + 16, NK - NCT)
                    dma("sp", vb[:, NCT + v0:NCT + v1, 0:66], vg_a[v0 * 128:v1 * 128, c * 66:(c + 1) * 66].rearrange("(n p) c -> p n c", p=128), r=[], w=[vk])
                subs = [(half * 64, 64, 0, c, 2 * c + half) for half in range(2)]
                scale = 64 ** -0.5
            else:
                cp(kb[:, 0:CTX], KTc[:, 1 + c, :], r=["KTc"], w=[kk])
                dma("sp", kb[:, CTX:LK], ktd_a[c * 128:(c + 1) * 128, :], r=[], w=[kk])
                cp(vb[:, 0:NCT, :], Vc[:, :, 132 + c * 132:132 + (c + 1) * 132], r=["Vc"], w=[vk])
                for v0 in range(0, NK - NCT, 16):
                    v1 = min(v0 + 16, NK - NCT)
                    dma("sp", vb[:, NCT + v0:NCT + v1, :], vd_a[v0 * 128:v1 * 128, c * 132:(c + 1) * 132].rearrange("(n p) c -> p n c", p=128), r=[], w=[vk])
                subs = [(i * 32, 32, i // 2, 2 + c, (2 * c + i // 2, i % 2)) for i in range(4)]
                scale = 32 ** -0.5
            for (pb_, dd, vs, qc, hid) in subs:
                tp = (96, 0) if pb_ == 96 else None
                for (qs, q0, qn, nk) in ((1, 0, CTX, NCT), (0, CTX, T, NK)):
                    for (o, bw) in blocks_of(qn, QB):
                        psO, pkO = psum(poolO)
                        for kt in range(nk):
                            psS, pkS = psum(poolS)
                            mm(psS[:, :bw], kb[pb_:pb_ + dd, kt * 128:(kt + 1) * 128], QT[pb_:pb_ + dd, qc, q0 + o:q0 + o + bw], True, True,
                               r=[kk, "QT"], w=[pkS], tp=tp)
                            pt = PTb[ptc % 4]; ptk = ("PT", ptc % 4); ptc += 1
                            act(pt[:, :bw], psS[:, :bw], AF.Exp, r=[pkS], w=[ptk], scale=scale)
                            mm(psO[0:66, :bw], vb[:, kt, vs * 66:(vs + 1) * 66], pt[:, :bw], kt == 0, kt == nk - 1, r=[vk, ptk], w=[pkO])
                        ot = OTb[otc % 2]; otk = ("OT", otc % 2); otc += 1
                        cp(ot[:, :bw], psO[0:66, :bw], r=[pkO], w=[otk], eng="act")
                        for tq in range(bw // 128):
                            t = (q0 + o) // 128 + tq
                            psT, pkT = psum(poolT)
                            tr(psT[:, 0:66], ot[:, tq * 128:(tq + 1) * 128], ident[:66, :66], r=[otk, "ident"], w=[pkT])
                            recip(rinv, psT[:, 64:65], r=[pkT], w=["rinv"])
                            if kind == "g":
                                ts(mix_tm[:, t, hid * 64:(hid + 1) * 64], psT[:, 0:64], rinv[:, 0:1], ALU.mult, r=[pkT, "rinv"], w=[("mix_tm", t)])
                            else:
                                hh, j = hid
                                if j == 0:
                                    ts(o1[:, t, :], psT[:, 0:64], rinv[:, 0:1], ALU.mult, r=[pkT, "rinv"], w=[("o1", t)])
                                else:
                                    ts(otmp, psT[:, 0:64], rinv[:, 0:1], ALU.mult, r=[pkT, "rinv"], w=["otmp"])
                                    stt(otmp, otmp, nlam[:, 0:1], o1[:, t, :], ALU.mult, ALU.add, r=["otmp", "nlam", ("o1", t)], w=["otmp"])
                                    act(osq, otmp, AF.Square, r=["otmp"], w=["osq", "oss"], accum=oss)
                                    act(oss, oss, AF.Sqrt, r=["oss"], w=["oss"], bias=EPS, scale=1.0 / 64)
                                    recip(oss, oss, r=["oss"], w=["oss"])
                                    stt(mix_tm[:, t, 256 + hh * 64:256 + (hh + 1) * 64], otmp, oss[:, 0:1], sublns, ALU.mult, ALU.mult,
                                        r=["otmp", "oss", "sublns"], w=[("mix_tm", t)])
        for t in range(NTT):
            ps, pk = psum(); pb = ps.bitcast(BF16)
            for j in range(4):
                tr(pb[:, j * 128:(j + 1) * 128], mix_tm[:, t, j * 128:(j + 1) * 128], identb, r=[("mix_tm", t), "identb"], w=[pk])
            cp(um[:, 2:4, t * 128:(t + 1) * 128], pb[:, 0:256].rearrange("p (c t) -> p c t", c=2), r=[pk], w=["mixT"], eng="act")
            cp(um[:, 6:8, t * 128:(t + 1) * 128], pb[:, 256:512].rearrange("p (c t) -> p c t", c=2), r=[pk], w=["mixT"])
        dump("mixT", um[:, :, 0:TT], "mixT")
        P.barrier()
        A.release(m8)
        A.release(m_qt)

        wo = A.alloc([128, KC, D], BF16)
        for kc in range(KC):
            dma("pool", wo[:, kc, :], wout_d[kc * 128:(kc + 1) * 128, :], r=[], w=["wo"])
        hb = [A.alloc([128, KC, 512], F32) for _ in range(2)]
        nscr = ([A.alloc([128, 512], F32) for _ in range(2)], A.alloc([128, 512], F32),
                [A.alloc([128, 512], F32) for _ in range(2)])
        vT = A.alloc([128, KC, 512], BF16)
        h1T = A.alloc([128, 32, 512], BF16)
        w1b = [A.alloc([128, KC, 512], BF16) for _ in range(2)]
        w2b = [A.alloc([128, 32, 128], BF16) for _ in range(2)]
        rt = [A.alloc([128, 512], F32) for _ in range(2)]
        bi = 0; w1c = 0; w2c = 0; rc = 0
        for (s, c0, n) in ((1, 0, CTX), (0, CTX, T)):
            for (o, bw) in blocks_of(n):
                h = hb[bi % 2]; hk = ("hb", bi % 2); bi += 1
                tcol = c0 + o
                dma("sp", h[:, :, :bw], hT_d[:, :, tcol:tcol + bw].rearrange("k p t -> p k t"), r=[], w=[hk])
                for nn in range(KC):
                    ps, pk = psum()
                    for kc in range(KC):
                        mm(ps[:, :bw], wo[:, kc, nn * 128:(nn + 1) * 128], um[:, kc, tcol:tcol + bw], kc == 0, kc == KC - 1, r=["wo", "mixT"], w=[pk])
                    stt(h[:, nn, :bw], ps[:, :bw], gt1[:, nn, s:s + 1], h[:, nn, :bw], ALU.mult, ALU.add, r=[pk, hk, "mod"], w=[hk])
                norm_block(h, hk, bw, s, gs2, "gs2", sh2, vT, "vT", nscr)
                for fg in range(8):
                    wb = w1b[w1c % 2]; wk = ("w1b", w1c % 2); w1c += 1
                    for kc in range(KC):
                        dma("pool", wb[:, kc, :], w1_d[kc * 128:(kc + 1) * 128, fg * 512:(fg + 1) * 512], r=[], w=[wk])
                    for f4 in range(4):
                        f = fg * 4 + f4
                        ps, pk = psum()
                        for kc in range(KC):
                            mm(ps[:, :bw], wb[:, kc, f4 * 128:(f4 + 1) * 128], vT[:, kc, :bw], kc == 0, kc == KC - 1, r=[wk, "vT"], w=[pk])
                        r_ = rt[rc % 2]; rk_ = ("rt", rc % 2); rc += 1
                        act(r_[:, :bw], ps[:, :bw], AF.Relu, r=[pk], w=[rk_])
                        tt(h1T[:, f, :bw], r_[:, :bw], r_[:, :bw], ALU.mult, r=[rk_], w=[("h1T", f)])
                for nn in range(KC):
                    wb = w2b[w2c % 2]; wk = ("w2b", w2c % 2); w2c += 1
                    dma("pool", wb, w2_d[:, nn * 128:(nn + 1) * 128].rearrange("(f p) c -> p f c", p=128), r=[], w=[wk])
                    ps, pk = psum()
                    for f in range(32):
                        mm(ps[:, :bw], wb[:, f, :], h1T[:, f, :bw], f == 0, f == 31, r=[wk, ("h1T", f)], w=[pk])
                    stt(h[:, nn, :bw], ps[:, :bw], gt2[:, nn, s:s + 1], h[:, nn, :bw], ALU.mult, ALU.add, r=[pk, hk, "mod"], w=[hk])
                dma("sp", hout_d[:, :, tcol:tcol + bw].rearrange("k p t -> p k t"), h[:, :, :bw], r=[hk], w=["hout"])
        P.finish(["hout"] + [("dbg", n) for n in dbg_out])
        P.emit(block, csem, dsem)
    return nc, dins, douts


def _rope_tables(S):
    pos = np.arange(S)
    row = (pos // GRID_W).astype(np.float32); col = (pos % GRID_W).astype(np.float32)

    def cs(p, half):
        freqs = (np.float32(10000.0) ** (-np.arange(half, dtype=np.float32) / np.float32(half))).astype(np.float32)
        ang = (p[:, None] * freqs[None, :]).astype(np.float32)
        return np.cos(ang).astype(np.float32), np.sin(ang).astype(np.float32)

    def tab(hd, nh):
        half = hd // 4
        cr, sr = cs(row, half); cc, sc = cs(col, half)
        C = np.concatenate([cr, cr, cc, cc], axis=1)
        Sg = np.concatenate([-sr, sr, -sc, sc], axis=1)
        return np.tile(C, (1, nh)), np.tile(Sg, (1, nh))
    C6, S6 = tab(64, 6); C16, S16 = tab(32, 16)
    return np.concatenate([C6, S6, C16, S16], axis=1).astype(np.float32)


def _bc(v, n=128):
    return np.ascontiguousarray(np.broadcast_to(np.asarray(v, np.float32).reshape(1, -1), (n, np.asarray(v).size)))


_PROG_CACHE = {}


def _get_prog(T, phase, dbg=None):
    key = (T, phase, tuple(sorted(dbg.items())) if dbg else None)
    if key not in _PROG_CACHE:
        _PROG_CACHE[key] = build_program(T, phase, dbg)
    return _PROG_CACHE[key]


def _layer_inputs(l, inp, lam_init):
    f = lambda a: np.ascontiguousarray(np.asarray(a, np.float32))
    d = {}
    d["w_mod"] = f(inp["w_mod"][l]); d["b_modT"] = f(inp["b_mod"][l].reshape(48, 128).T)
    d["g1"] = f(inp["norm1_g"][l].reshape(KC, 128).T); d["g2"] = f(inp["norm2_g"][l].reshape(KC, 128).T)
    d["w_in"] = f(inp["w_in"][l])
    d["convw_s"] = f(inp["ssd_conv_w"][l].T.reshape(6, 128, 4).transpose(1, 0, 2))
    d["convb_s"] = f(inp["ssd_conv_b"][l].reshape(6, 128).T)
    d["convw_l"] = f(inp["lru_conv_w"][l].T.reshape(2, 128, 4).transpose(1, 0, 2))
    d["convb_l"] = f(inp["lru_conv_b"][l].reshape(2, 128).T)
    d["alog"] = f(inp["ssd_a_log"][l].reshape(8, 1)); d["dtb"] = f(inp["ssd_dt_bias"][l].reshape(8, 1))
    d["ssd_d_bc"] = f(np.broadcast_to(np.repeat(inp["ssd_d"][l], 64, axis=1)[None], (128, 2, 256)))
    d["ssd_ng_bc"] = _bc(inp["ssd_norm_g"][l])
    d["gqk_g"] = _bc(np.concatenate([np.tile(inp["gqa_q_norm_g"][l], 4), np.tile(inp["gqa_k_norm_g"][l], 2)]))
    d["dqk_g"] = _bc(np.concatenate([np.tile(inp["diff_q_norm_g"][l], 8), np.tile(inp["diff_k_norm_g"][l], 8)]))
    lv = np.stack([inp["diff_lambda_q1"][l], inp["diff_lambda_k1"][l], inp["diff_lambda_q2"][l], inp["diff_lambda_k2"][l]])
    d["lamv"] = f(np.broadcast_to(lv[None], (128, 4, 32)))
    d["laminit"] = np.full((128, 1), lam_init, np.float32)
    d["subln_bc"] = _bc(inp["diff_subln_g"][l])
    lw = np.zeros((128, 8, 128), np.float32)
    for dr in range(2):
        for ri, wname in enumerate(("lru_w_r", "lru_w_i")):
            w = np.asarray(inp[wname][l][dr], np.float32)
            for c in range(2):
                for bb in range(2):
                    lw[bb * 64:(bb + 1) * 64, dr * 4 + ri * 2 + c, bb * 64:(bb + 1) * 64] = w[2 * c + bb]
    d["lru_w"] = lw
    lb = np.zeros((128, 8), np.float32)
    for dr in range(2):
        for ri, bname in enumerate(("lru_b_r", "lru_b_i")):
            for c in range(2):
                lb[:, dr * 4 + ri * 2 + c] = inp[bname][l][dr][c * 128:(c + 1) * 128]
    d["lru_b"] = lb
    ll = np.zeros((128, 4), np.float32)
    for dr in range(2):
        for c in range(2):
            ll[:, dr * 2 + c] = inp["lru_lambda"][l][dr][c * 128:(c + 1) * 128]
    d["lru_lam"] = ll
    d["w_out"] = f(inp["w_out"][l]); d["w_mlp1"] = f(inp["w_mlp1"][l]); d["w_mlp2"] = f(inp["w_mlp2"][l])
    return d


def run_model(inp, depth=DEPTH, dbg=None, dbg_layer=0):
    x = np.asarray(inp["x"], np.float32)
    B, S, _ = x.shape
    T = S // 4
    TT = CTX + T
    NT = T // 128
    n = 8
    ctx = np.asarray(inp["ctx"], np.float32)
    rope_full = _rope_tables(S)
    tri = np.arange(128)
    consts = {"ident": np.eye(128, dtype=np.float32),
              "maskf": (tri[None, :] >= tri[:, None]).astype(np.float32),
              "maskb": (tri[None, :] <= tri[:, None]).astype(np.float32)}
    hT = []
    for core in range(n):
        b, q = divmod(core, 4)
        hc = np.concatenate([ctx[b], x[b, q * T:(q + 1) * T]], axis=0)
        hT.append(np.ascontiguousarray(hc.T.reshape(KC, 128, TT)))
    cT = []
    for core in range(n):
        b = core // 4
        cc = np.stack([np.asarray(inp["c"], np.float32)[b], np.asarray(inp["c_ctx"], np.float32)], axis=1)
        cT.append(np.ascontiguousarray(cc.reshape(KC, 128, 2).transpose(1, 0, 2)))
    dbg_res = None
    n_l = 0
    for l in range(depth):
        lam_init = 0.8 - 0.6 * math.exp(-0.3 * l)
        li = _layer_inputs(l, inp, lam_init)
        in_maps = []
        for core in range(n):
            b, q = divmod(core, 4)
            m = dict(li); m.update(consts)
            m["hT"] = hT[core]; m["cT"] = cT[core]
            halo = np.zeros((KC, 128, 4), np.float32); hm = np.zeros((128, 4), np.float32)
            if q > 0:
                halo[:, :, 0:2] = hT[core - 1][:, :, TT - 2:TT]; hm[:, 0:2] = 1.0
            if q < 3:
                halo[:, :, 2] = hT[core + 1][:, :, CTX]; hm[:, 2] = 1.0
            m["halo"] = halo; m["hmask"] = hm
            m["rope"] = np.ascontiguousarray(rope_full[q * T:(q + 1) * T].reshape(NT, 128, 1792))
            fm = np.zeros((128, 8), np.float32)
            for j in range(4):
                fm[:, j] = 1.0 if j < q else 0.0
                fm[:, 4 + j] = 1.0 if j > q else 0.0
            m["foldmask"] = fm
            in_maps.append(m)
        ncA, _, _ = _get_prog(T, "A")
        resA = run_bass_kernel_spmd(ncA, in_maps, core_ids=list(range(n))).results
        for core in range(n):
            b = core // 4
            grp = [resA[b * 4 + j] for j in range(4)]
            m = in_maps[core]
            m["KTg_all"] = np.ascontiguousarray(np.concatenate([g["KTg"] for g in grp], axis=1))
            m["Vg_all"] = np.ascontiguousarray(np.concatenate([g["Vg"] for g in grp], axis=0))
            m["KTd_all"] = np.ascontiguousarray(np.concatenate([g["KTd"] for g in grp], axis=1))
            m["Vd_all"] = np.ascontiguousarray(np.concatenate([g["Vd"] for g in grp], axis=0))
            m["ssd_ex_all"] = np.ascontiguousarray(np.stack([g["ssd_ex"] for g in grp]))
            m["lru_ex_all"] = np.ascontiguousarray(np.stack([g["lru_ex"] for g in grp]))
        use_dbg = dbg if (dbg and l == dbg_layer) else None
        ncB, _, _ = _get_prog(T, "B", use_dbg)
        resB = run_bass_kernel_spmd(ncB, in_maps, core_ids=list(range(n))).results
        if use_dbg:
            dbg_res = resB
        hT = [np.asarray(resB[core]["hT_out"], np.float32) for core in range(n)]
    out = np.zeros((B, S, D), np.float32)
    for core in range(n):
        b, q = divmod(core, 4)
        out[b, q * T:(q + 1) * T] = hT[core].reshape(D, TT).T[CTX:]
    if dbg:
        return out, dbg_res
    return out


def kernel(**inputs):
    return run_model(inputs)
```
